# Optimizing a Trainium2 kernel written in Bass

```python
import jax, jax.numpy as jnp
from jax import lax
import numpy as np

D_MODEL = 2048
BATCH = 4
SEQ = 2048
DEPTH = 4

EPS = 1e-6
SSM_EXPAND = 2
SSM_D_INNER = SSM_EXPAND * D_MODEL
SSM_HEAD_DIM = 64
SSM_HEADS = SSM_D_INNER // SSM_HEAD_DIM
SSM_GROUPS = 8
SSM_STATE = 128
SSM_CONV = 4
SSM_CHUNK = 128
SSM_CONV_DIM = SSM_D_INNER + 2 * SSM_GROUPS * SSM_STATE
ATTN_HEADS = 16
ATTN_HEAD_DIM = 128
ATTN_KV_HEADS = 4
ATTN_WIDTH = ATTN_HEADS * ATTN_HEAD_DIM
ATTN_KV_WIDTH = ATTN_KV_HEADS * ATTN_HEAD_DIM
IDX_HEADS = 16
IDX_HEAD_DIM = 64
TOPK_MAX = 256
Q_BLOCK = 128
ROPE_THETA = 500000.0
ATTN_ROT_DIM = ATTN_HEAD_DIM // 4
IDX_ROT_DIM = IDX_HEAD_DIM // 4
FFN_HIDDEN = -(-8 * D_MODEL // (3 * 256)) * 256

IN_SPLITS = (
    SSM_D_INNER,
    SSM_CONV_DIM,
    SSM_HEADS,
    ATTN_WIDTH,
    ATTN_KV_WIDTH,
    ATTN_KV_WIDTH,
    IDX_HEADS * IDX_HEAD_DIM,
    IDX_HEAD_DIM,
    IDX_HEADS,
    D_MODEL,
    D_MODEL,
)
IN_COLS = int(sum(IN_SPLITS))
IN_OFFSETS = tuple(int(o) for o in np.cumsum(IN_SPLITS)[:-1])

kernel_name = "hybrid_ssd_dsa_gated_block"


def rmsnorm(x, g):
    xf = x.astype(jnp.float32)
    xf = xf * lax.rsqrt(jnp.mean(xf * xf, axis=-1, keepdims=True) + EPS)
    return xf.astype(x.dtype) * g


def rope_tables(positions, rot_dim):
    inv_freq = ROPE_THETA ** (-(jnp.arange(0, rot_dim, 2, dtype=jnp.float32) / rot_dim))
    ang = positions.astype(jnp.float32)[..., None] * inv_freq
    return jnp.cos(ang), jnp.sin(ang)


def apply_partial_rope(x, cos, sin):
    rot = 2 * cos.shape[-1]
    x1, x2, xp = x[..., : rot // 2], x[..., rot // 2: rot], x[..., rot:]
    c = cos[:, :, None, :].astype(x.dtype)
    s = sin[:, :, None, :].astype(x.dtype)
    return jnp.concatenate([x1 * c - x2 * s, x2 * c + x1 * s, xp], axis=-1)


def causal_dwconv(x, w, b):
    k = w.shape[0]
    xp = jnp.pad(x, ((0, 0), (k - 1, 0), (0, 0)))
    y = lax.conv_general_dilated(xp, w[:, None, :].astype(x.dtype), window_strides=(1,),
                                 padding="VALID", dimension_numbers=("NWC", "WIO", "NWC"),
                                 feature_group_count=x.shape[-1])
    return y + b


def segsum(a):
    cs = jnp.cumsum(a, axis=-1)
    t = a.shape[-1]
    diff = cs[..., :, None] - cs[..., None, :]
    mask = jnp.tril(jnp.ones((t, t), dtype=bool))
    return jnp.where(mask, diff, -jnp.inf)


def ssd_chunked(x, dt, a, bm, cm):
    b, s, h, p = x.shape
    g, n = bm.shape[2], bm.shape[3]
    r = h // g
    nc, lc = s // SSM_CHUNK, SSM_CHUNK
    xd = (x.astype(jnp.float32) * dt[..., None]).reshape(b, nc, lc, g, r, p)
    adt = (dt * a).reshape(b, nc, lc, g, r).transpose(0, 3, 4, 1, 2)
    bc = bm.astype(jnp.float32).reshape(b, nc, lc, g, n)
    cc = cm.astype(jnp.float32).reshape(b, nc, lc, g, n)
    a_cs = jnp.cumsum(adt, axis=-1)
    lmat = jnp.exp(segsum(adt))
    cb = jnp.einsum("bclgn,bcsgn->bcgls", cc, bc)
    y_diag = jnp.einsum("bcgls,bgrcls,bcsgrp->bclgrp", cb, lmat, xd)
    decay = jnp.exp(a_cs[..., -1:] - a_cs)
    states = jnp.einsum("bcsgn,bgrcs,bcsgrp->bcgrpn", bc, decay, xd)
    chunk_decay = jnp.exp(a_cs[..., -1])

    def step(hstate, inp):
        st, dec = inp
        return dec[..., None, None] * hstate + st, hstate

    h0 = jnp.zeros((b, g, r, p, n), jnp.float32)
    _, prev = lax.scan(step, h0, (jnp.moveaxis(states, 1, 0), jnp.moveaxis(chunk_decay, -1, 0)))
    prev = jnp.moveaxis(prev, 0, 1)
    y_off = jnp.einsum("bclgn,bcgrpn,bgrcl->bclgrp", cc, prev, jnp.exp(a_cs))
    return (y_diag + y_off).reshape(b, s, h, p)


def mamba_branch(z, xbc, dt_raw, conv_w, conv_b, dt_bias, a_log, d_skip, norm_g):
    b, s = z.shape[:2]
    xbc = jax.nn.silu(causal_dwconv(xbc, conv_w, conv_b))
    xs, bm, cm = jnp.split(xbc, [SSM_D_INNER, SSM_D_INNER + SSM_GROUPS * SSM_STATE], axis=-1)
    xs = xs.reshape(b, s, SSM_HEADS, SSM_HEAD_DIM)
    bm = bm.reshape(b, s, SSM_GROUPS, SSM_STATE)
    cm = cm.reshape(b, s, SSM_GROUPS, SSM_STATE)
    dt = jax.nn.softplus(dt_raw.astype(jnp.float32) + dt_bias.astype(jnp.float32))
    a = -jnp.exp(a_log.astype(jnp.float32))
    y = ssd_chunked(xs, dt, a, bm, cm) + d_skip.astype(jnp.float32)[:, None] * xs.astype(jnp.float32)
    y = y.reshape(b, s, SSM_D_INNER) * jax.nn.silu(z.astype(jnp.float32))
    yg = y.reshape(b, s, SSM_GROUPS, SSM_D_INNER // SSM_GROUPS)
    yg = yg * lax.rsqrt(jnp.mean(yg * yg, axis=-1, keepdims=True) + EPS)
    return yg.reshape(b, s, SSM_D_INNER).astype(z.dtype) * norm_g


def dsa_branch(q, k, v, q_idx, k_idx, w_idx, idx_k_norm, cos_a, sin_a, cos_i, sin_i):
    b, s = q.shape[:2]
    rep = ATTN_HEADS // ATTN_KV_HEADS
    q = apply_partial_rope(q.reshape(b, s, ATTN_HEADS, ATTN_HEAD_DIM), cos_a, sin_a)
    k = apply_partial_rope(k.reshape(b, s, ATTN_KV_HEADS, ATTN_HEAD_DIM), cos_a, sin_a)
    v = v.reshape(b, s, ATTN_KV_HEADS, ATTN_HEAD_DIM)
    q_idx = apply_partial_rope(q_idx.reshape(b, s, IDX_HEADS, IDX_HEAD_DIM), cos_i, sin_i)
    k_idx = apply_partial_rope(rmsnorm(k_idx, idx_k_norm)[:, :, None, :], cos_i, sin_i)[:, :, 0, :]
    w_idx = w_idx.astype(jnp.float32) * (IDX_HEADS ** -0.5 * IDX_HEAD_DIM ** -0.5)
    top_k = min(TOPK_MAX, s // 4)
    nb = s // Q_BLOCK
    q_blk = jnp.moveaxis(q.reshape(b, nb, Q_BLOCK, ATTN_KV_HEADS, rep, ATTN_HEAD_DIM), 1, 0)
    qi_blk = jnp.moveaxis(q_idx.reshape(b, nb, Q_BLOCK, IDX_HEADS, IDX_HEAD_DIM), 1, 0)
    w_blk = jnp.moveaxis(w_idx.reshape(b, nb, Q_BLOCK, IDX_HEADS), 1, 0)
    starts = jnp.arange(nb, dtype=jnp.int32) * Q_BLOCK
    key_pos = jnp.arange(s, dtype=jnp.int32)
    scale = ATTN_HEAD_DIM ** -0.5

    def block(args):
        qb, qib, wb, start = args
        qpos = start + jnp.arange(Q_BLOCK, dtype=jnp.int32)
        causal = key_pos[None, :] <= qpos[:, None]
        logits = jnp.einsum("bqhd,bsd->bqhs", qib, k_idx).astype(jnp.float32)
        score = jnp.einsum("bqhs,bqh->bqs", jax.nn.relu(logits), wb)
        score = jnp.where(causal[None], score, -jnp.inf)
        _, idx = lax.top_k(score, top_k)
        k_sel = jax.vmap(lambda kk, ii: kk[ii])(k, idx)
        v_sel = jax.vmap(lambda vv, ii: vv[ii])(v, idx)
        valid = idx <= qpos[None, :, None]
        sc = jnp.einsum("bqgrd,bqkgd->bqgrk", qb, k_sel).astype(jnp.float32) * scale
        sc = jnp.where(valid[:, :, None, None, :], sc, -jnp.inf)
        p = jax.nn.softmax(sc, axis=-1).astype(v.dtype)
        o = jnp.einsum("bqgrk,bqkgd->bqgrd", p, v_sel)
        return o.reshape(b, Q_BLOCK, ATTN_WIDTH)

    out = lax.map(block, (q_blk, qi_blk, w_blk, starts))
    return jnp.moveaxis(out, 0, 1).reshape(b, s, ATTN_WIDTH)


def setup_inputs(seed: int = 0) -> dict:
    key = jax.random.key(seed)
    ks = jax.random.split(key, 20)
    f32 = jnp.float32

    def nrm(k, shape, fan_in):
        return jax.random.normal(k, shape, f32) * (fan_in ** -0.5)

    def gain(k, shape):
        return 1.0 + 0.02 * jax.random.normal(k, shape, f32)

    x = jax.random.normal(ks[0], (BATCH, SEQ, D_MODEL), f32)
    start = jax.random.randint(ks[1], (BATCH, 1), 0, 4096, dtype=jnp.int32)
    positions = (start + jnp.arange(SEQ, dtype=jnp.int32)[None, :]).astype(jnp.int32)
    dt0 = jnp.exp(jax.random.uniform(ks[6], (DEPTH, SSM_HEADS), f32, np.log(1e-3), np.log(1e-1)))
    dt_bias = dt0 + jnp.log(-jnp.expm1(-dt0))
    return {
        "x": x,
        "positions": positions,
        "mix_norm": gain(ks[2], (DEPTH, D_MODEL)),
        "w_in": nrm(ks[3], (DEPTH, D_MODEL, IN_COLS), D_MODEL),
        "conv_w": nrm(ks[4], (DEPTH, SSM_CONV, SSM_CONV_DIM), SSM_CONV),
        "conv_b": 0.02 * jax.random.normal(ks[5], (DEPTH, SSM_CONV_DIM), f32),
        "dt_bias": dt_bias,
        "a_log": jnp.log(jax.random.uniform(ks[7], (DEPTH, SSM_HEADS), f32, 1.0, 16.0)),
        "d_skip": 1.0 + 0.1 * jax.random.normal(ks[8], (DEPTH, SSM_HEADS), f32),
        "ssm_norm": gain(ks[9], (DEPTH, SSM_D_INNER)),
        "idx_k_norm": gain(ks[10], (DEPTH, IDX_HEAD_DIM)),
        "w_proj_a": nrm(ks[11], (DEPTH, SSM_D_INNER, D_MODEL), SSM_D_INNER),
        "w_proj_b": nrm(ks[12], (DEPTH, ATTN_WIDTH, D_MODEL), ATTN_WIDTH),
        "w_out": nrm(ks[13], (DEPTH, D_MODEL, D_MODEL), D_MODEL),
        "ffn_norm": gain(ks[14], (DEPTH, D_MODEL)),
        "w_ffn_gate": nrm(ks[15], (DEPTH, D_MODEL, FFN_HIDDEN), D_MODEL),
        "w_ffn_up": nrm(ks[16], (DEPTH, D_MODEL, FFN_HIDDEN), D_MODEL),
        "w_ffn_down": nrm(ks[17], (DEPTH, FFN_HIDDEN, D_MODEL), FFN_HIDDEN),
        "final_norm": gain(ks[18], (D_MODEL,)),
    }


def reference(x, positions, mix_norm, w_in, conv_w, conv_b, dt_bias, a_log, d_skip, ssm_norm,
              idx_k_norm, w_proj_a, w_proj_b, w_out, ffn_norm, w_ffn_gate, w_ffn_up, w_ffn_down,
              final_norm):
    cos_a, sin_a = rope_tables(positions, ATTN_ROT_DIM)
    cos_i, sin_i = rope_tables(positions, IDX_ROT_DIM)
    for i in range(DEPTH):
        u = rmsnorm(x, mix_norm[i])
        proj = jnp.einsum("bsd,dc->bsc", u, w_in[i])
        (z, xbc, dt_raw, q, k, v, q_idx, k_idx, w_idx,
         g_a, g_b) = jnp.split(proj, IN_OFFSETS, axis=-1)
        y_a = mamba_branch(z, xbc, dt_raw, conv_w[i], conv_b[i], dt_bias[i], a_log[i],
                           d_skip[i], ssm_norm[i])
        y_b = dsa_branch(q, k, v, q_idx, k_idx, w_idx, idx_k_norm[i], cos_a, sin_a, cos_i, sin_i)
        merged = (jax.nn.sigmoid(g_a) * jnp.einsum("bse,ed->bsd", y_a, w_proj_a[i])
                  + jax.nn.sigmoid(g_b) * jnp.einsum("bse,ed->bsd", y_b, w_proj_b[i]))
        x = x + jnp.einsum("bsd,de->bse", merged, w_out[i])
        h = rmsnorm(x, ffn_norm[i])
        ff = jax.nn.silu(jnp.einsum("bsd,df->bsf", h, w_ffn_gate[i])) * jnp.einsum("bsd,df->bsf", h, w_ffn_up[i])
        x = x + jnp.einsum("bsf,fd->bsd", ff, w_ffn_down[i])
    return rmsnorm(x, final_norm)
```

```python
import math
from contextlib import ExitStack
import numpy as np
import concourse.bass as bass
import concourse.mybir as mybir
from concourse.bass_utils import run_bass_kernel_spmd

F32 = mybir.dt.float32
BF16 = mybir.dt.bfloat16
I32 = mybir.dt.int32
AF = mybir.ActivationFunctionType
ALU = mybir.AluOpType
AX = mybir.AxisListType

D_MODEL = 2048
DEPTH = 4
EPS = 1e-6
D_INNER = 4096
SSM_HEADS = 64
SSM_GROUPS = 8
SSM_STATE = 128
CONV_DIM = D_INNER + 2 * SSM_GROUPS * SSM_STATE
ATTN_HEADS = 16
HEAD_DIM = 128
KV_HEADS = 4
IDX_HEADS = 16
IDX_DIM = 64
FFN_HIDDEN = 5632
IN_SPLITS = (D_INNER, CONV_DIM, SSM_HEADS, 2048, 512, 512, IDX_HEADS * IDX_DIM, IDX_DIM, IDX_HEADS, D_MODEL, D_MODEL)
IN_COLS = int(sum(IN_SPLITS))
OFF = [0] + [int(o) for o in np.cumsum(IN_SPLITS)]
(O_Z, O_XBC, O_DT, O_Q, O_K, O_V, O_QI, O_KI, O_WI, O_GA, O_GB) = OFF[:11]
ROPE_THETA = 500000.0
NDC = D_MODEL // 128


class Tok:
    __slots__ = ("key", "val", "snap")

    def __init__(self, key, val, snap):
        self.key, self.val, self.snap = key, val, snap


class TR:
    __slots__ = ("w", "r", "name", "excl")

    def __init__(self, name="", excl=False):
        self.w = {}
        self.r = {}
        self.name = name
        self.excl = excl


class FW:
    COMPUTE = ("pe", "act", "dve", "pool")

    def __init__(self, nc, es):
        self.nc = nc
        self.es = es
        self.eng = {"pe": nc.tensor, "act": nc.scalar, "dve": nc.vector, "pool": nc.gpsimd, "sp": nc.sync}
        self.sems = {}
        for e in self.COMPUTE:
            self.sems[e] = es.enter_context(nc.semaphore("s_" + e))
        self.cnt = {e: 0 for e in self.COMPUTE}
        self.known = {e: {} for e in self.eng}
        self.nd = 8
        self.dring = {}
        for q in ("sp", "act", "pool"):
            self.dring[q] = []
            for i in range(self.nd):
                key = ("d", q, i)
                self.sems[key] = es.enter_context(nc.semaphore("d_%s%d" % (q, i)))
                self.dring[q].append([key, 0])
        self.dpos = {q: 0 for q in self.dring}
        self.nwaits = 0
        self.nops = 0

    def _need(self, e, tok):
        kn = self.known[e]
        if kn.get(tok.key, 0) >= tok.val:
            return
        self.eng[e].wait_ge(self.sems[tok.key], tok.val)
        self.nwaits += 1
        for k, v in tok.snap.items():
            if kn.get(k, 0) < v:
                kn[k] = v
        kn[tok.key] = tok.val

    def _deps(self, e, reads, writes):
        for t in reads:
            for k, tok in t.w.items():
                if k == e and e == "pe":
                    continue
                self._need(e, tok)
            if t.excl:
                for k, tok in t.r.items():
                    if k != e:
                        self._need(e, tok)
        for t in writes:
            for k, tok in t.w.items():
                if k == e:
                    continue
                self._need(e, tok)
            for k, tok in t.r.items():
                if k == e:
                    continue
                self._need(e, tok)

    def _mark(self, key, tok, reads, writes):
        for t in reads:
            t.r[key] = tok
        for t in writes:
            if t.r:
                t.w = {key: tok}
                t.r = {}
            else:
                t.w[key] = tok

    def op(self, e, fn, reads=(), writes=()):
        self._deps(e, reads, writes)
        ins = fn(self.eng[e])
        self.cnt[e] += 1
        ins.then_inc(self.sems[e], 1)
        tok = Tok(e, self.cnt[e], dict(self.known[e]))
        self._mark(e, tok, reads, writes)
        self.nops += 1
        return tok

    def new_dsem(self, name):
        key = ("dd", name)
        self.sems[key] = self.es.enter_context(self.nc.semaphore("dd_" + name))
        return [key, 0]

    def dma(self, q, out, in_, reads=(), writes=(), ent=None, **kw):
        self._deps(q, reads, writes)
        if ent is None:
            ent = self.dring[q][self.dpos[q]]
            self.dpos[q] = (self.dpos[q] + 1) % self.nd
            if ent[1] > 0:
                self._need(q, Tok(ent[0], ent[1], {}))
        key = ent[0]
        ent[1] += 16
        ins = self.eng[q].dma_start(out=out, in_=in_, **kw)
        ins.then_inc(self.sems[key], 16)
        tok = Tok(key, ent[1], dict(self.known[q]))
        self._mark(key, tok, reads, writes)
        self.nops += 1
        return tok

    def collective_allgather(self, src_h, dst_h, ncores, key, val, reads=(), writes=()):
        e = "pool"
        self._deps(e, reads, writes)
        ins = self.eng[e].collective_compute("AllGather", ALU.bypass, replica_groups=[list(range(ncores))],
                                             ins=[src_h.ap().opt()], outs=[dst_h.ap().opt()])
        ins.then_inc(self.sems[key], 1)
        tok = Tok(key, val, dict(self.known[e]))
        self._mark(key, tok, reads, writes)
        self.nops += 1
        return tok

    def barrier(self):
        toks = [Tok(e, self.cnt[e], {}) for e in self.COMPUTE if self.cnt[e] > 0]
        for q in self.dring:
            for ent in self.dring[q]:
                if ent[1] > 0:
                    toks.append(Tok(ent[0], ent[1], {}))
        for e in self.eng:
            for tok in toks:
                if tok.key == e and e == "pe":
                    continue
                self._need(e, tok)

    def wait_all(self, e, trackers):
        for t in trackers:
            for tok in t.w.values():
                self._need(e, tok)
            for tok in t.r.values():
                self._need(e, tok)


class Ctx:
    pass


_UID = [0]
_CUT = [0]


def U(name):
    _UID[0] += 1
    return "%s_%d" % (name, _UID[0])


def build_program(S=2048, nlayers=DEPTH, stages=("ssd", "dsa", "merge", "ffn"), debug=(), ncores=1):
    nc = bass.Bass("TRN2", target_bir_lowering=False)
    es = ExitStack()
    fw = FW(nc, es)
    NT = S // 512
    c = Ctx()
    c.nc, c.fw, c.S, c.NT = nc, fw, S, NT
    c.debug = debug
    c.cut = _CUT[0]

    def dram_in(name, shape, dt=F32):
        return nc.dram_tensor(name, list(shape), dt, kind="ExternalInput").ap()

    xT_in = dram_in("xT", [D_MODEL, S])
    c.mix_norm = dram_in("mix_norm", [DEPTH, 128, NDC])
    c.ffn_norm = dram_in("ffn_norm", [DEPTH, 128, NDC])
    c.final_norm = dram_in("final_norm", [128, NDC])
    NCR = ncores
    c.w_gate = dram_in("w_ffn_gate", [DEPTH, D_MODEL // NCR, FFN_HIDDEN])
    c.w_up = dram_in("w_ffn_up", [DEPTH, D_MODEL // NCR, FFN_HIDDEN])
    c.w_down = dram_in("w_ffn_down", [DEPTH, FFN_HIDDEN // NCR, D_MODEL])
    c.w_in = dram_in("w_in", [DEPTH, D_MODEL // NCR, IN_COLS])
    c.w_pa = dram_in("w_proj_a", [DEPTH, D_INNER // NCR, D_MODEL])
    c.w_pb = dram_in("w_proj_b", [DEPTH, 2048 // NCR, D_MODEL])
    c.w_out = dram_in("w_out", [DEPTH, D_MODEL // NCR, D_MODEL])
    c.conv_w = dram_in("conv_w", [DEPTH, 128, 48, 4])
    c.conv_b = dram_in("conv_b", [DEPTH, 128, 48])
    c.dt_bias = dram_in("dt_bias", [DEPTH, 128, 64])
    c.a_log = dram_in("a_log", [DEPTH, 128, 64])
    c.d_skip = dram_in("d_skip", [DEPTH, 128, 32])
    c.ssm_norm = dram_in("ssm_norm", [DEPTH, 128, 32])
    c.idx_k_norm = dram_in("idx_k_norm", [DEPTH, 128, 1])
    c.pos_in = dram_in("positions", [128, S], I32)
    c.consts_in = dram_in("consts", [128, NCONST])
    outT = nc.dram_tensor("outT", [D_MODEL, S], F32, kind="ExternalOutput").ap()
    c.dbg = {}
    for name, shape, dt in debug:
        c.dbg[name] = nc.dram_tensor("dbg_" + name, list(shape), dt, kind="ExternalOutput").ap()

    c.wfull = {}
    c.xT = nc.dram_tensor("xT_res", [D_MODEL, S], F32).ap()
    c.xT_tr = [[TR("xT%d_%d" % (dc, tt)) for tt in range(NT)] for dc in range(NDC)]
    _h = [nc.dram_tensor("wg_bf%d" % i, [D_MODEL, FFN_HIDDEN], BF16) for i in range(DEPTH)]
    c.wg_bf = [h.ap() for h in _h]
    for i in range(DEPTH):
        c.wfull[("wg", i)] = _h[i]
    _h = [nc.dram_tensor("wu_bf%d" % i, [D_MODEL, FFN_HIDDEN], BF16) for i in range(DEPTH)]
    c.wu_bf = [h.ap() for h in _h]
    for i in range(DEPTH):
        c.wfull[("wu", i)] = _h[i]
    _h = [nc.dram_tensor("wd_bf%d" % i, [FFN_HIDDEN, D_MODEL], BF16) for i in range(DEPTH)]
    c.wd_bf = [h.ap() for h in _h]
    for i in range(DEPTH):
        c.wfull[("wd", i)] = _h[i]
    _h = [nc.dram_tensor("win_bf%d" % i, [D_MODEL, IN_COLS], BF16) for i in range(DEPTH)]
    c.win_bf = [h.ap() for h in _h]
    for i in range(DEPTH):
        c.wfull[("win", i)] = _h[i]
    _h = [nc.dram_tensor("wpa_bf%d" % i, [D_INNER, D_MODEL], BF16) for i in range(DEPTH)]
    c.wpa_bf = [h.ap() for h in _h]
    for i in range(DEPTH):
        c.wfull[("wpa", i)] = _h[i]
    _h = [nc.dram_tensor("wpb_bf%d" % i, [2048, D_MODEL], BF16) for i in range(DEPTH)]
    c.wpb_bf = [h.ap() for h in _h]
    for i in range(DEPTH):
        c.wfull[("wpb", i)] = _h[i]
    _h = [nc.dram_tensor("wout_bf%d" % i, [D_MODEL, D_MODEL], BF16) for i in range(DEPTH)]
    c.wout_bf = [h.ap() for h in _h]
    for i in range(DEPTH):
        c.wfull[("wout", i)] = _h[i]
    if "yaT" in c.dbg:
        c.yaT = c.dbg["yaT"]
    else:
        c.yaT = nc.dram_tensor("yaT", [D_INNER, S], BF16).ap()
    c.yaT_tr = TR("yaT")
    if "ybT" in c.dbg:
        c.ybT = c.dbg["ybT"]
    else:
        c.ybT = nc.dram_tensor("ybT", [2048, S], BF16).ap()
    c.ybT_tr = TR("ybT")
    c.qT_d = nc.dram_tensor("qT_d", [2048, S], BF16).ap()
    c.qiT_d = nc.dram_tensor("qiT_d", [1024, S], BF16).ap()
    c.qT_tr = TR("qT_d")
    c.wtr = {}

    c.consts = nc.alloc_sbuf_tensor("consts_sb", [128, NCONST], F32)
    c.const_tr = TR("consts")
    fw.dma("sp", c.consts[:], c.consts_in, writes=[c.const_tr])
    c.T_f = c.consts[:, 0:128]
    c.ident_f = c.consts[:, 256:384]
    c.cbf = nc.alloc_sbuf_tensor("cbf", [128, NCONST], BF16)
    fw.op("dve", lambda e: e.tensor_copy(out=c.cbf[:], in_=c.consts[:]), reads=[c.const_tr], writes=[c.const_tr])
    c.T_bf = c.cbf[:, 0:128]
    c.mneg_bf = c.cbf[:, 128:256]
    c.ident_bf = c.cbf[:, 256:384]
    c.ones_bf = c.cbf[:, 384:512]
    c.PA_bf = c.cbf[:, 768:896]
    c.PI_bf = c.cbf[:, 896:1024]
    c.ones_f = c.consts[:, 384:512]
    c.ones_tr = c.const_tr
    c.eps_t = nc.alloc_sbuf_tensor("eps_t", [128, 1], F32)
    c.one_t = nc.alloc_sbuf_tensor("one_t", [128, 1], F32)
    c.eps_tr = TR("eps")
    fw.op("pool", lambda e: e.memset(c.eps_t[:], EPS), writes=[c.eps_tr])
    fw.op("pool", lambda e: e.memset(c.one_t[:], 1.0), writes=[c.eps_tr])
    c.gains = nc.alloc_sbuf_tensor("gains", [128, 2 * DEPTH + 1, NDC], F32)
    c.gains_tr = TR("gains")
    for l in range(DEPTH):
        fw.dma("sp", c.gains[:, 2 * l, :], c.mix_norm[l], writes=[c.gains_tr])
        fw.dma("sp", c.gains[:, 2 * l + 1, :], c.ffn_norm[l], writes=[c.gains_tr])
    fw.dma("sp", c.gains[:, 2 * DEPTH, :], c.final_norm, writes=[c.gains_tr])

    c.ps = [nc.alloc_psum_tensor("ps%d" % i, [128, 512], F32) for i in range(7)]
    c.ps_bf = nc.alloc_psum_tensor("ps7", [128, 1024], BF16)
    c.ps_tr = [TR("ps%d" % i, excl=True) for i in range(8)]

    mix = any(s in stages for s in ("ssd", "dsa", "merge"))
    c.cast_ent = fw.new_dsem("cast")
    c.cast_tr = TR("cast")

    def convert_layer(l):
        lst = []
        if mix:
            lst += [("win", c.w_in, c.win_bf), ("wpa", c.w_pa, c.wpa_bf), ("wpb", c.w_pb, c.wpb_bf), ("wout", c.w_out, c.wout_bf)]
        if "ffn" in stages:
            lst += [("wg", c.w_gate, c.wg_bf), ("wu", c.w_up, c.wu_bf), ("wd", c.w_down, c.wd_bf)]
        key = ("cc", "L%d" % l)
        fw.sems[key] = es.enter_context(nc.semaphore("cc_L%d" % l))
        ncc = 0
        bounces = []
        for nm, src, dst in lst:
            t = TR("%s%d" % (nm, l))
            c.wtr[(nm, l)] = t
            R = src.shape[1]
            C = src.shape[2]
            if ncores == 1:
                tgt, tt_ = dst[l], t
            else:
                bounce = nc.dram_tensor("%s_sh%d" % (nm, l), [R, C], BF16)
                tgt, tt_ = bounce.ap(), c.cast_tr
                bounces.append((nm, bounce, t))
            nsp = 4 if ncores == 1 else 1
            ncs = 4 if nm == "win" else 1
            for i in range(nsp):
                r0, r1 = i * R // nsp, (i + 1) * R // nsp
                for jc in range(ncs):
                    c0, c1 = jc * C // ncs, (jc + 1) * C // ncs
                    fw.dma("pool", tgt[r0:r1, c0:c1], src[l, r0:r1, c0:c1], writes=[tt_], ent=c.cast_ent)
        for nm, bounce, t in bounces:
            fw.collective_allgather(bounce, c.wfull[(nm, l)], ncores, key, len(bounces), reads=[c.cast_tr], writes=[t])

    for l in range(min(2, nlayers)):
        convert_layer(l)

    for dc in range(NDC):
        fw.dma("sp", c.xT[dc * 128:(dc + 1) * 128, :], xT_in[dc * 128:(dc + 1) * 128, :], writes=c.xT_tr[dc])

    if "dsa" in stages:
        rope_tables(c)
    for l in range(nlayers):
        if l >= 1 and l + 1 < nlayers:
            convert_layer(l + 1)
        if mix:
            with nc.sbuf_tensor(U("uT"), [128, NDC, S], BF16) as uT_h:
                c.uT = uT_h
                c.uT_tr = [TR("uT%d" % tt) for tt in range(NT)]
                with ExitStack() as es2:
                    norm_bufs(c, es2)
                    for tt in range(NT):
                        norm_tile(c, tt, 2 * l, dst=lambda dc, tt=tt: c.uT[:, dc, tt * 512:(tt + 1) * 512], dst_tr=c.uT_tr[tt])
                fw.barrier()
                if "ssd" in stages:
                    ssd_stage(c, l)
                    fw.barrier()
                if "dsa" in stages:
                    dsa_stage(c, l)
                    fw.barrier()
                if "merge" in stages:
                    merge_stage(c, l)
                    fw.barrier()
        if "ffn" in stages:
            with nc.sbuf_tensor(U("actT"), [128, NDC, 512], BF16) as actT_h:
                c.actT = actT_h
                c.actT_tr = TR("actT")
                ffn_layer(c, l)
                fw.barrier()
    final_norm_stage(c, outT)
    fw.wait_all("sp", c.out_trs + [c.yaT_tr, c.ybT_tr])
    c.es = es
    return nc, c


NCONST = 1056
NIT = 20


def make_consts():
    k = np.zeros((128, NCONST), np.float32)
    i = np.arange(128)
    k[:, 0:128] = (i[:, None] <= i[None, :])
    k[:, 128:256] = np.where(i[None, :] < i[:, None], -30000.0, 0.0)
    k[:, 256:384] = np.eye(128)
    k[:, 384:512] = 1.0
    k[:, 512:640] = np.where(i[None, :] <= i[:, None], 0.0, -1e30)
    k[:, 640:768] = (i[:, None] <= i[None, :])
    PA = np.zeros((128, 128), np.float32)
    for m in range(32):
        PA[m + 16 if m < 16 else m - 16, m] = 1.0
    PI = np.zeros((128, 128), np.float32)
    for base in (0, 64):
        for m in range(16):
            PI[base + (m + 8 if m < 8 else m - 8), base + m] = 1.0
    k[:, 768:896] = PA
    k[:, 896:1024] = PI
    invA = np.zeros(128); sgnA = np.zeros(128); invI = np.zeros(128); sgnI = np.zeros(128)
    for p in range(32):
        invA[p] = ROPE_THETA ** (-(2.0 * (p % 16)) / 32.0)
        sgnA[p] = -1.0 if p < 16 else 1.0
    for base in (0, 64):
        for m in range(16):
            invI[base + m] = ROPE_THETA ** (-(2.0 * (m % 8)) / 16.0)
            sgnI[base + m] = -1.0 if m < 8 else 1.0
    k[:, 1024] = invA.astype(np.float32)
    k[:, 1025] = sgnA
    k[:, 1026] = invI.astype(np.float32)
    k[:, 1027] = sgnI
    k[:, 1028:1028 + NIT] = (0.5 ** (np.arange(NIT) + 1.0))[None, :]
    return k


def norm_tile(c, tt, gidx, dst=None, dst_tr=None, out_f32_dram=None):
    nc, fw = c.nc, c.fw
    xt, xt_tr = c.n_xt, c.n_xt_tr
    sq, sq_tr = c.n_sq, c.n_sq_tr
    rstd, rstd_tr = c.n_rstd, c.n_rstd_tr
    pi = 0
    ps, ps_tr = c.ps[pi], c.ps_tr[pi]
    for dc in range(NDC):
        fw.dma("sp", xt[:, dc, :], c.xT[dc * 128:(dc + 1) * 128, tt * 512:(tt + 1) * 512],
               reads=[c.xT_tr[dc][tt]], writes=[xt_tr[dc]])
    for dc in range(NDC):
        fw.op("act", lambda e, dc=dc: e.activation(out=sq[:, dc % 2, :], in_=xt[:, dc, :], func=AF.Square),
              reads=[xt_tr[dc]], writes=[sq_tr[dc % 2]])
        fw.op("pe", lambda e, dc=dc: e.matmul(ps[:], c.ones_f[:], sq[:, dc % 2, :], start=(dc == 0), stop=(dc == NDC - 1)),
              reads=[sq_tr[dc % 2], c.ones_tr], writes=[ps_tr])
    fw.op("act", lambda e: e.activation(out=rstd[:], in_=ps[:], func=AF.Sqrt, bias=c.eps_t[:], scale=1.0 / D_MODEL),
          reads=[ps_tr, c.eps_tr], writes=[rstd_tr])
    fw.op("dve", lambda e: e.reciprocal(out=rstd[:], in_=rstd[:]), reads=[rstd_tr], writes=[rstd_tr])
    for dc in range(NDC):
        if out_f32_dram is None:
            eng = "dve"
            fw.op(eng, lambda e, dc=dc: e.scalar_tensor_tensor(
                out=dst(dc), in0=xt[:, dc, :], scalar=c.gains[:, gidx, dc:dc + 1], in1=rstd[:],
                op0=ALU.mult, op1=ALU.mult),
                reads=[xt_tr[dc], rstd_tr, c.gains_tr], writes=[dst_tr])
        else:
            eng = "dve"
            fw.op(eng, lambda e, dc=dc: e.scalar_tensor_tensor(
                out=xt[:, dc, :], in0=xt[:, dc, :], scalar=c.gains[:, gidx, dc:dc + 1], in1=rstd[:],
                op0=ALU.mult, op1=ALU.mult),
                reads=[xt_tr[dc], rstd_tr, c.gains_tr], writes=[xt_tr[dc]])
            t = TR()
            c.out_trs.append(t)
            fw.dma("sp", out_f32_dram[dc * 128:(dc + 1) * 128, tt * 512:(tt + 1) * 512], xt[:, dc, :],
                   reads=[xt_tr[dc]], writes=[t])


def norm_bufs(c, es):
    nc = c.nc
    c.n_xt = es.enter_context(nc.sbuf_tensor(U("n_xt"), [128, NDC, 512], F32))
    c.n_xt_tr = [TR("n_xt%d" % i) for i in range(NDC)]
    c.n_sq = es.enter_context(nc.sbuf_tensor(U("n_sq"), [128, 2, 512], F32))
    c.n_sq_tr = [TR("n_sq0"), TR("n_sq1")]
    c.n_rstd = es.enter_context(nc.sbuf_tensor(U("n_rstd"), [128, 512], F32))
    c.n_rstd_tr = TR("n_rstd")


def final_norm_stage(c, outT):
    c.out_trs = []
    with ExitStack() as es:
        norm_bufs(c, es)
        for tt in range(c.NT):
            norm_tile(c, tt, 2 * DEPTH, out_f32_dram=outT)


def ffn_layer(c, l):
    nc, fw = c.nc, c.fw
    NFC = FFN_HIDDEN // 128
    WG = 2
    WD = 2
    with ExitStack() as es:
        norm_bufs(c, es)
        ffT = es.enter_context(nc.sbuf_tensor(U("ffT"), [128, NFC, 512], BF16))
        ff_tr = [TR("ff%d" % i) for i in range(NFC)]
        wg = [es.enter_context(nc.sbuf_tensor(U("wg%d" % i), [128, NDC, 128 * WG], BF16)) for i in range(2)]
        wu = [es.enter_context(nc.sbuf_tensor(U("wu%d" % i), [128, NDC, 128 * WG], BF16)) for i in range(2)]
        wg_tr = [TR(), TR()]
        wu_tr = [TR(), TR()]
        wd = [es.enter_context(nc.sbuf_tensor(U("wd%d" % i), [128, NFC, 128 * WD], BF16)) for i in range(2)]
        wd_tr = [TR(), TR()]
        sil = es.enter_context(nc.sbuf_tensor(U("sil"), [128, 2, 512], F32))
        sil_tr = [TR(), TR()]
        xo = es.enter_context(nc.sbuf_tensor(U("xo"), [128, 2, 512], F32))
        xo_tr = [TR(), TR()]
        it = 0
        it2 = 0
        for tt in range(c.NT):
            norm_tile(c, tt, 2 * l + 1, dst=lambda dc: c.actT[:, dc, :], dst_tr=c.actT_tr)
            for fg in range(NFC // WG):
                b = it % 2
                it += 1
                cols = slice(fg * 128 * WG, (fg + 1) * 128 * WG)
                fw.dma("sp", wg[b][:], c.wg_bf[l][:, cols].rearrange("(c p) f -> p c f", p=128),
                       reads=[c.wtr[("wg", l)]], writes=[wg_tr[b]])
                fw.dma("sp", wu[b][:], c.wu_bf[l][:, cols].rearrange("(c p) f -> p c f", p=128),
                       reads=[c.wtr[("wu", l)]], writes=[wu_tr[b]])
                for j in range(WG):
                    fc = fg * WG + j
                    pg, pu = 1 + (fc % 2) * 2, 2 + (fc % 2) * 2
                    for dc in range(NDC):
                        fw.op("pe", lambda e, dc=dc, j=j, pg=pg, b=b: e.matmul(
                            c.ps[pg][:], wg[b][:, dc, j * 128:(j + 1) * 128], c.actT[:, dc, :],
                            start=(dc == 0), stop=(dc == NDC - 1)),
                            reads=[wg_tr[b], c.actT_tr], writes=[c.ps_tr[pg]])
                    for dc in range(NDC):
                        fw.op("pe", lambda e, dc=dc, j=j, pu=pu, b=b: e.matmul(
                            c.ps[pu][:], wu[b][:, dc, j * 128:(j + 1) * 128], c.actT[:, dc, :],
                            start=(dc == 0), stop=(dc == NDC - 1)),
                            reads=[wu_tr[b], c.actT_tr], writes=[c.ps_tr[pu]])
                    sb = fc % 2
                    fw.op("act", lambda e, pg=pg, sb=sb: e.activation(out=sil[:, sb, :], in_=c.ps[pg][:], func=AF.Silu),
                          reads=[c.ps_tr[pg]], writes=[sil_tr[sb]])
                    fw.op("dve", lambda e, pu=pu, sb=sb, fc=fc: e.tensor_tensor(
                        out=ffT[:, fc, :], in0=sil[:, sb, :], in1=c.ps[pu][:], op=ALU.mult),
                        reads=[sil_tr[sb], c.ps_tr[pu]], writes=[ff_tr[fc]])
            for dg in range(NDC // WD):
                b = it2 % 2
                it2 += 1
                cols = slice(dg * 128 * WD, (dg + 1) * 128 * WD)
                fw.dma("sp", wd[b][:], c.wd_bf[l][:, cols].rearrange("(c p) f -> p c f", p=128),
                       reads=[c.wtr[("wd", l)]], writes=[wd_tr[b]])
                for j in range(WD):
                    dc = dg * WD + j
                    po = 5 + dc % 2
                    ob = dc % 2
                    fw.dma("sp", xo[:, ob, :], c.xT[dc * 128:(dc + 1) * 128, tt * 512:(tt + 1) * 512],
                           reads=[c.xT_tr[dc][tt]], writes=[xo_tr[ob]])
                    for fc in range(NFC):
                        fw.op("pe", lambda e, fc=fc, j=j, po=po, b=b: e.matmul(
                            c.ps[po][:], wd[b][:, fc, j * 128:(j + 1) * 128], ffT[:, fc, :],
                            start=(fc == 0), stop=(fc == NFC - 1)),
                            reads=[wd_tr[b], ff_tr[fc]], writes=[c.ps_tr[po]])
                    fw.op("dve", lambda e, po=po, ob=ob: e.tensor_tensor(
                        out=xo[:, ob, :], in0=xo[:, ob, :], in1=c.ps[po][:], op=ALU.add),
                        reads=[xo_tr[ob], c.ps_tr[po]], writes=[xo_tr[ob]])
                    fw.dma("sp", c.xT[dc * 128:(dc + 1) * 128, tt * 512:(tt + 1) * 512], xo[:, ob, :],
                           reads=[xo_tr[ob]], writes=[c.xT_tr[dc][tt]])


def prep_inputs(inputs, b, S=2048, core=0, ncores=1):
    f = np.float32
    d = {}
    d["xT"] = np.ascontiguousarray(np.asarray(inputs["x"], f)[b, :S, :].T)

    def g(v, n=NDC):
        v = np.asarray(v, f)
        return np.ascontiguousarray(v.reshape(v.shape[:-1] + (n, 128)).swapaxes(-1, -2))
    d["mix_norm"] = g(inputs["mix_norm"])
    d["ffn_norm"] = g(inputs["ffn_norm"])
    d["final_norm"] = g(inputs["final_norm"])
    for k in ("w_ffn_gate", "w_ffn_up", "w_ffn_down", "w_in", "w_proj_a", "w_proj_b", "w_out"):
        w = np.asarray(inputs[k], f)
        if ncores > 1:
            R = w.shape[1] // ncores
            w = w[:, core * R:(core + 1) * R, :]
        d[k] = np.ascontiguousarray(w)
    cw = np.asarray(inputs["conv_w"], f)
    d["conv_w"] = np.ascontiguousarray(cw.reshape(DEPTH, 4, 48, 128).transpose(0, 3, 2, 1))
    d["conv_b"] = g(inputs["conv_b"], 48)
    rep = lambda v: np.ascontiguousarray(np.broadcast_to(np.asarray(v, f)[:, None, :], (DEPTH, 128, v.shape[-1])))
    d["dt_bias"] = rep(inputs["dt_bias"])
    d["a_log"] = rep(inputs["a_log"])
    d["d_skip"] = g(np.repeat(np.asarray(inputs["d_skip"], f), 64, axis=-1), 32)
    d["ssm_norm"] = g(inputs["ssm_norm"], 32)
    ikn = np.asarray(inputs["idx_k_norm"], f)
    d["idx_k_norm"] = np.ascontiguousarray(np.concatenate([ikn, ikn], axis=-1)[:, :, None])
    pos = np.asarray(inputs["positions"], np.int32)[b, :S]
    d["positions"] = np.ascontiguousarray(np.broadcast_to(pos[None, :], (128, S)))
    d["consts"] = make_consts()
    return d


def load_w(c, l, segs, wt, wt_tr):
    fw = c.fw
    off = 0
    for (c0, n) in segs:
        fw.dma("sp", wt[:, :, off:off + n], c.win_bf[l][:, c0:c0 + n].rearrange("(c p) f -> p c f", p=128),
               reads=[c.wtr[("win", l)]], writes=[wt_tr])
        off += n


def proj_fm(c, wt, wt_tr, j0, n, tt, ps_ap, ps_tr):
    fw = c.fw
    for dc in range(NDC):
        fw.op("pe", lambda e, dc=dc: e.matmul(ps_ap, wt[:, dc, j0:j0 + n], c.uT[:, dc, tt * 512:(tt + 1) * 512],
                                              start=(dc == 0), stop=(dc == NDC - 1)),
              reads=[wt_tr, c.uT_tr[tt]], writes=[ps_tr])


def ssd_stage(c, l):
    nc, fw, S, NT = c.nc, c.fw, c.S, c.NT
    NCK = S // 128
    H = SSM_HEADS
    with ExitStack() as es:
        def sb(name, shape, dt):
            return es.enter_context(nc.sbuf_tensor(U(name), shape, dt))
        dtb = sb("dtb", [128, H], F32)
        alog = sb("alog", [128, H], F32)
        dsk = sb("dsk", [128, 32], F32)
        sng = sb("sng", [128, 32], F32)
        cw = sb("cw", [128, 48, 4], F32)
        cb = sb("cb", [128, 48], F32)
        par_tr = TR("ssdpar")
        fw.dma("sp", dtb[:], c.dt_bias[l], writes=[par_tr])
        fw.dma("sp", alog[:], c.a_log[l], writes=[par_tr])
        fw.dma("sp", dsk[:], c.d_skip[l], writes=[par_tr])
        fw.dma("sp", sng[:], c.ssm_norm[l], writes=[par_tr])
        fw.dma("sp", cw[:], c.conv_w[l], writes=[par_tr])
        fw.dma("sp", cb[:], c.conv_b[l], writes=[par_tr])
        dt = sb("dt", [128, NCK, H], F32)
        adt_hi = sb("adt_hi", [128, NCK, H], BF16)
        adt_lo = sb("adt_lo", [128, NCK, H], BF16)
        nacs = sb("nacs", [128, NCK, H], F32)
        dtdec = sb("dtdec", [128, NCK, H], F32)
        cdec = sb("cdec", [128, NCK, H], F32)
        tmpa = sb("tmpa", [128, NCK, H], F32)
        adt = tmpa
        dt_tr, adt_tr, adth_tr, nacs_tr, dtdec_tr, cdec_tr, tmpa_tr = (TR() for _ in range(7))
        wdt = sb("wdt", [128, NDC, H], BF16)
        wdt_tr = TR()
        load_w(c, l, [(O_DT, H)], wdt, wdt_tr)
        NB = min(8, NCK)
        for b0 in range(0, NCK, NB):
            ps, ps_tr = c.ps[0], c.ps_tr[0]
            for bi in range(NB):
                ck = b0 + bi
                for dc in range(NDC):
                    fw.op("pe", lambda e, dc=dc, ck=ck, bi=bi: e.matmul(
                        ps[:, bi * H:(bi + 1) * H], c.uT[:, dc, ck * 128:(ck + 1) * 128], wdt[:, dc, :],
                        start=(dc == 0), stop=(dc == NDC - 1)),
                        reads=[wdt_tr, c.uT_tr[ck // 4]], writes=[ps_tr])
            fw.op("dve", lambda e, b0=b0: e.tensor_tensor(
                out=dt[:, b0:b0 + NB, :], in0=ps[:, 0:NB * H].rearrange("p (b h) -> p b h", h=H),
                in1=dtb[:].unsqueeze(1).to_broadcast([128, NB, H]), op=ALU.add),
                reads=[ps_tr, par_tr], writes=[dt_tr])
        if getattr(c, "cut", 0) == 11:
            return
        fw.op("act", lambda e: e.activation(out=dt[:], in_=dt[:], func=AF.Exp), reads=[dt_tr], writes=[dt_tr])
        fw.op("act", lambda e: e.activation(out=dt[:], in_=dt[:], func=AF.Ln, bias=c.one_t[:], scale=1.0),
              reads=[dt_tr, c.eps_tr], writes=[dt_tr])
        if getattr(c, "cut", 0) == 12:
            return
        fw.op("act", lambda e: e.activation(out=alog[:], in_=alog[:], func=AF.Exp), reads=[par_tr], writes=[par_tr])
        fw.op("dve", lambda e: e.scalar_tensor_tensor(
            out=adt[:], in0=dt[:], scalar=-1.0, in1=alog[:].unsqueeze(1).to_broadcast([128, NCK, H]),
            op0=ALU.mult, op1=ALU.mult), reads=[dt_tr, par_tr], writes=[adt_tr])
        fw.op("dve", lambda e: e.tensor_copy(out=adt_hi[:], in_=adt[:]), reads=[adt_tr], writes=[adth_tr])
        fw.op("dve", lambda e: e.tensor_tensor(out=adt_lo[:], in0=adt[:], in1=adt_hi[:], op=ALU.subtract),
              reads=[adt_tr, adth_tr], writes=[adth_tr])
        if getattr(c, "cut", 0) == 13:
            return
        for b0 in range(0, NCK, NB):
            for bi in range(NB):
                ck = b0 + bi
                for part, st, sp in ((adt_hi, True, False), (adt_lo, False, True)):
                    fw.op("pe", lambda e, ck=ck, bi=bi, part=part, st=st, sp=sp: e.matmul(
                        c.ps[0][:, bi * H:(bi + 1) * H], c.T_bf[:], part[:, ck, :], start=st, stop=sp),
                        reads=[adth_tr, c.const_tr], writes=[c.ps_tr[0]])
                for part, st, sp in ((adt_hi, True, False), (adt_lo, False, True)):
                    fw.op("pe", lambda e, ck=ck, bi=bi, part=part, st=st, sp=sp: e.matmul(
                        c.ps[1][:, bi * H:(bi + 1) * H], c.ones_bf[:], part[:, ck, :], start=st, stop=sp),
                        reads=[adth_tr, c.const_tr], writes=[c.ps_tr[1]])
            sl = slice(b0, b0 + NB)
            fw.op("act", lambda e, sl=sl: e.mul(out=nacs[:, sl, :], in_=c.ps[0][:, 0:NB * H].rearrange("p (b h) -> p b h", h=H), mul=-1.0),
                  reads=[c.ps_tr[0]], writes=[nacs_tr])
            fw.op("act", lambda e, sl=sl: e.activation(out=cdec[:, sl, :], in_=c.ps[1][:, 0:NB * H].rearrange("p (b h) -> p b h", h=H), func=AF.Exp),
                  reads=[c.ps_tr[1]], writes=[cdec_tr])
            fw.op("dve", lambda e, sl=sl: e.tensor_tensor(out=tmpa[:, sl, :], in0=c.ps[1][:, 0:NB * H].rearrange("p (b h) -> p b h", h=H),
                                                          in1=nacs[:, sl, :], op=ALU.add),
                  reads=[c.ps_tr[1], nacs_tr], writes=[tmpa_tr])
        if getattr(c, "cut", 0) == 14:
            return
        fw.op("act", lambda e: e.activation(out=tmpa[:], in_=tmpa[:], func=AF.Exp), reads=[tmpa_tr], writes=[tmpa_tr])
        fw.op("dve", lambda e: e.tensor_tensor(out=dtdec[:], in0=tmpa[:], in1=dt[:], op=ALU.mult),
              reads=[tmpa_tr, dt_tr], writes=[dtdec_tr])

        if getattr(c, "cut", 0) == 1:
            return
        pre = sb("pre", [128, S + 3], F32)
        acc = sb("acc", [128, S], F32)
        pre_tr, acc_tr = TR(), TR()
        fw.op("pool", lambda e: e.memset(pre[:, 0:3], 0.0), writes=[pre_tr])
        xc = sb("xc", [128, 4, S], BF16)
        BT = sb("BT", [128, S], BF16)
        CT = sb("CT", [128, S], BF16)
        zs = sb("zs", [128, 4, S], BF16)
        xc_tr = [TR() for _ in range(4)]
        BT_tr, CT_tr = TR(), TR()
        zs_tr = [TR() for _ in range(4)]
        ya = [sb("ya%d" % i, [128, 4, 128], BF16) for i in range(2)]
        ya_tr = [TR(), TR()]
        wts = [sb("wt%d" % i, [128, NDC, 256], BF16) for i in range(2)]
        wts_tr = [TR(), TR()]
        state = sb("state", [128, 8, 64], F32)
        prev_bf = sb("prev_bf", [128, 8, 64], BF16)
        state_tr, prev_tr = TR(), TR()
        xd = [sb("xd%d" % i, [128, 8, 64], BF16) for i in range(2)]
        xdd = [sb("xdd%d" % i, [128, 8, 64], BF16) for i in range(2)]
        xd_tr = [TR(), TR()]
        xdd_tr = [TR(), TR()]
        Btm = [sb("Btm%d" % i, [128, 128], BF16) for i in range(2)]
        Btm_tr = [TR(), TR()]
        CBt = [sb("CBt%d" % i, [128, 128], BF16) for i in range(2)]
        CBt_tr = [TR(), TR()]
        E2 = [sb("E2%d" % i, [128, 4, 128], BF16) for i in range(2)]
        Lm = [sb("Lm%d" % i, [128, 4, 128], BF16) for i in range(2)]
        Mt = [sb("Mt%d" % i, [128, 4, 128], BF16) for i in range(2)]
        Cs = [sb("Cs%d" % i, [128, 4, 128], BF16) for i in range(2)]
        E2_tr, Lm_tr, Mt_tr, Cs_tr = ([TR(), TR()] for _ in range(4))
        yg = sb("yg", [128, 4, 128], F32)
        ysq = sb("ysq", [128, 4, 128], BF16)
        yrs = sb("yrs", [128, 128], F32)
        yg_tr, ysq_tr, yrs_tr = TR(), TR(), TR()
        tpb = c.ps_bf
        tpb_tr = c.ps_tr[7]
        wi = 0
        for g in range(SSM_GROUPS):
            chunks = [("z", j, O_Z + g * 512 + j * 128) for j in range(4)]
            chunks += [("x", j, O_XBC + g * 512 + j * 128) for j in range(4)]
            chunks += [("B", 0, O_XBC + D_INNER + g * 128), ("C", 0, O_XBC + D_INNER + 1024 + g * 128)]
            for ci in range(0, 10, 2):
                w = wi % 2
                wi += 1
                load_w(c, l, [(chunks[ci][2], 128), (chunks[ci + 1][2], 128)], wts[w], wts_tr[w])
                for k in range(2):
                    kind, j, col0 = chunks[ci + k]
                    cidx = {"x": g * 4 + j, "B": 32 + g, "C": 40 + g}.get(kind, 0)
                    for tt in range(NT):
                        pi = (tt + k) % 2
                        proj_fm(c, wts[w], wts_tr[w], k * 128, 128, tt, c.ps[pi][:], c.ps_tr[pi])
                        tsl = slice(tt * 512, (tt + 1) * 512)
                        if kind == "z":
                            fw.op("act", lambda e, pi=pi, j=j, tsl=tsl: e.activation(out=zs[:, j, tsl], in_=c.ps[pi][:], func=AF.Silu),
                                  reads=[c.ps_tr[pi]], writes=[zs_tr[j]])
                        else:
                            fw.op("act", lambda e, pi=pi, tt=tt: e.copy(out=pre[:, 3 + tt * 512:3 + (tt + 1) * 512], in_=c.ps[pi][:]),
                                  reads=[c.ps_tr[pi]], writes=[pre_tr])
                    if kind == "z":
                        continue
                    fw.op("dve", lambda e, cidx=cidx: e.tensor_scalar(out=acc[:], in0=pre[:, 3:3 + S], scalar1=cw[:, cidx, 3:4], scalar2=None, op0=ALU.mult),
                          reads=[pre_tr, par_tr], writes=[acc_tr])
                    for kk in (2, 1, 0):
                        fw.op("dve", lambda e, cidx=cidx, kk=kk: e.scalar_tensor_tensor(
                            out=acc[:], in0=pre[:, kk:kk + S], scalar=cw[:, cidx, kk:kk + 1], in1=acc[:],
                            op0=ALU.mult, op1=ALU.add), reads=[pre_tr, acc_tr, par_tr], writes=[acc_tr])
                    if kind == "x":
                        dst, dtr = xc[:, j, :], xc_tr[j]
                    elif kind == "B":
                        dst, dtr = BT[:], BT_tr
                    else:
                        dst, dtr = CT[:], CT_tr
                    fw.op("act", lambda e, dst=dst, cidx=cidx: e.activation(out=dst, in_=acc[:], func=AF.Silu, bias=cb[:, cidx:cidx + 1], scale=1.0),
                          reads=[acc_tr, par_tr], writes=[dtr])
            if getattr(c, "cut", 0) == 2:
                return
            fw.op("pool", lambda e: e.memset(state[:], 0.0), writes=[state_tr])
            for ck in range(NCK):
                b = ck % 2
                csl = slice(ck * 128, (ck + 1) * 128)
                hs = slice(g * 8, g * 8 + 8)
                for j in range(4):
                    fw.op("pe", lambda e, j=j, csl=csl: e.transpose(tpb[:, j * 128:(j + 1) * 128], xc[:, j, csl], c.ident_bf[:]),
                          reads=[xc_tr[j], c.const_tr], writes=[tpb_tr])
                fw.op("pe", lambda e, csl=csl: e.transpose(tpb[:, 512:640], BT[:, csl], c.ident_bf[:]),
                      reads=[BT_tr, c.const_tr], writes=[tpb_tr])
                if getattr(c, "cut", 0) == 312:
                    return
                fw.op("dve", lambda e, b=b, ck=ck, hs=hs: e.tensor_tensor(
                    out=xd[b][:], in0=tpb[:, 0:512].rearrange("p (h q) -> p h q", q=64),
                    in1=dt[:, ck, hs].unsqueeze(2).to_broadcast([128, 8, 64]), op=ALU.mult),
                    reads=[tpb_tr, dt_tr], writes=[xd_tr[b]])
                if getattr(c, "cut", 0) == 313:
                    return
                fw.op("dve", lambda e, b=b, ck=ck, hs=hs: e.tensor_tensor(
                    out=xdd[b][:], in0=tpb[:, 0:512].rearrange("p (h q) -> p h q", q=64),
                    in1=dtdec[:, ck, hs].unsqueeze(2).to_broadcast([128, 8, 64]), op=ALU.mult),
                    reads=[tpb_tr, dtdec_tr], writes=[xdd_tr[b]])
                if getattr(c, "cut", 0) == 314:
                    return
                fw.op("dve", lambda e, b=b: e.tensor_copy(out=Btm[b][:], in_=tpb[:, 512:640]), reads=[tpb_tr], writes=[Btm_tr[b]])
                if getattr(c, "cut", 0) == 31:
                    return
                fw.op("pe", lambda e, csl=csl: e.matmul(c.ps[3][:, 0:128], BT[:, csl], CT[:, csl], start=True, stop=True),
                      reads=[BT_tr, CT_tr], writes=[c.ps_tr[3]])
                fw.op("act", lambda e, b=b: e.copy(out=CBt[b][:], in_=c.ps[3][:, 0:128]), reads=[c.ps_tr[3]], writes=[CBt_tr[b]])
                if getattr(c, "cut", 0) == 32:
                    return
                for half in range(2):
                    hb = (ck * 2 + half) % 2
                    for q in range(4):
                        h = g * 8 + half * 4 + q
                        qs = slice(q * 128, (q + 1) * 128)
                        fw.op("pe", lambda e, ck=ck, h=h, qs=qs: e.matmul(c.ps[4][:, qs], adt_hi[:, ck, h:h + 1].to_broadcast([128, 128]), c.T_bf[:], start=True, stop=False),
                              reads=[adth_tr, c.const_tr], writes=[c.ps_tr[4]])
                        fw.op("pe", lambda e, ck=ck, h=h, qs=qs: e.matmul(c.ps[4][:, qs], adt_lo[:, ck, h:h + 1].to_broadcast([128, 128]), c.T_bf[:], start=False, stop=True),
                              reads=[adth_tr, c.const_tr], writes=[c.ps_tr[4]])
                        fw.op("pe", lambda e, ck=ck, h=h, qs=qs: e.matmul(c.ps[5][:, qs], adt_hi[:, ck, h:h + 1].to_broadcast([128, 128]), c.T_bf[:], start=True, stop=False),
                              reads=[adth_tr, c.const_tr], writes=[c.ps_tr[5]])
                        fw.op("pe", lambda e, ck=ck, h=h, qs=qs: e.matmul(c.ps[5][:, qs], adt_lo[:, ck, h:h + 1].to_broadcast([128, 128]), c.T_bf[:], start=False, stop=False),
                              reads=[adth_tr, c.const_tr], writes=[c.ps_tr[5]])
                        fw.op("pe", lambda e, qs=qs: e.matmul(c.ps[5][:, qs], c.ident_bf[:], c.mneg_bf[:], start=False, stop=True),
                              reads=[c.const_tr], writes=[c.ps_tr[5]])
                    if getattr(c, "cut", 0) == 33:
                        return
                    fw.op("act", lambda e, hb=hb: e.activation(out=E2[hb][:], in_=c.ps[4][:].rearrange("p (q l) -> p q l", l=128), func=AF.Exp),
                          reads=[c.ps_tr[4]], writes=[E2_tr[hb]])
                    for q in range(4):
                        h = g * 8 + half * 4 + q
                        fw.op("act", lambda e, hb=hb, q=q, ck=ck, h=h: e.activation(
                            out=Lm[hb][:, q, :], in_=c.ps[5][:, q * 128:(q + 1) * 128], func=AF.Exp,
                            bias=nacs[:, ck, h:h + 1], scale=1.0),
                            reads=[c.ps_tr[5], nacs_tr], writes=[Lm_tr[hb]])
                    if getattr(c, "cut", 0) == 34:
                        return
                    fw.op("dve", lambda e, hb=hb, b=b: e.tensor_tensor(
                        out=Mt[hb][:], in0=Lm[hb][:], in1=CBt[b][:].unsqueeze(1).to_broadcast([128, 4, 128]), op=ALU.mult),
                        reads=[Lm_tr[hb], CBt_tr[b]], writes=[Mt_tr[hb]])
                    if ck > 0:
                        fw.op("pool", lambda e, hb=hb, csl=csl: e.tensor_tensor(
                            out=Cs[hb][:], in0=E2[hb][:], in1=CT[:, csl].unsqueeze(1).to_broadcast([128, 4, 128]), op=ALU.mult),
                            reads=[E2_tr[hb], CT_tr], writes=[Cs_tr[hb]])
                    if getattr(c, "cut", 0) == 35:
                        return
                    for q in range(4):
                        hh = half * 4 + q
                        po = (hh % 2) * 64
                        cols = slice((hh // 2) * 128, (hh // 2) * 128 + 128)
                        fw.op("pe", lambda e, b=b, hb=hb, q=q, hh=hh, po=po, cols=cols, ck=ck: e.matmul(
                            c.ps[6][po:po + 64, cols], xd[b][:, hh, :], Mt[hb][:, q, :], start=True, stop=(ck == 0)),
                            reads=[xd_tr[b], Mt_tr[hb]], writes=[c.ps_tr[6]])
                        if ck > 0:
                            fw.op("pe", lambda e, hb=hb, q=q, hh=hh, po=po, cols=cols: e.matmul(
                                c.ps[6][po:po + 64, cols], prev_bf[:, hh, :], Cs[hb][:, q, :], start=False, stop=True),
                                reads=[prev_tr, Cs_tr[hb]], writes=[c.ps_tr[6]])
                if getattr(c, "cut", 0) == 3:
                    return
                fw.op("pe", lambda e, b=b: e.matmul(c.ps[2][:], Btm[b][:], xdd[b][:].rearrange("p h q -> p (h q)"), start=True, stop=True),
                      reads=[Btm_tr[b], xdd_tr[b]], writes=[c.ps_tr[2]])
                if ck < NCK - 1:
                    fw.op("dve", lambda e, ck=ck, hs=hs: e.tensor_tensor(
                        out=state[:], in0=state[:], in1=cdec[:, ck, hs].unsqueeze(2).to_broadcast([128, 8, 64]), op=ALU.mult),
                        reads=[state_tr, cdec_tr], writes=[state_tr])
                    fw.op("dve", lambda e: e.tensor_tensor(
                        out=state[:], in0=state[:], in1=c.ps[2][:].rearrange("p (h q) -> p h q", q=64), op=ALU.add),
                        reads=[state_tr, c.ps_tr[2]], writes=[state_tr])
                    fw.op("act", lambda e: e.copy(out=prev_bf[:], in_=state[:]), reads=[state_tr], writes=[prev_tr])
                if getattr(c, "cut", 0) == 4:
                    return
                for j in range(4):
                    fw.op("dve", lambda e, j=j, csl=csl: e.scalar_tensor_tensor(
                        out=yg[:, j, :], in0=xc[:, j, csl], scalar=dsk[:, g * 4 + j:g * 4 + j + 1], in1=c.ps[6][:, j * 128:(j + 1) * 128],
                        op0=ALU.mult, op1=ALU.add), reads=[xc_tr[j], par_tr, c.ps_tr[6]], writes=[yg_tr])
                fw.op("dve", lambda e, csl=csl: e.tensor_tensor(out=yg[:], in0=yg[:], in1=zs[:, :, csl], op=ALU.mult),
                      reads=[yg_tr] + zs_tr, writes=[yg_tr])
                fw.op("act", lambda e: e.activation(out=ysq[:], in_=yg[:], func=AF.Square), reads=[yg_tr], writes=[ysq_tr])
                for j in range(4):
                    fw.op("pe", lambda e, j=j: e.matmul(c.ps[3][:, 128:256], c.ones_bf[:], ysq[:, j, :], start=(j == 0), stop=(j == 3)),
                          reads=[ysq_tr, c.const_tr], writes=[c.ps_tr[3]])
                fw.op("act", lambda e: e.activation(out=yrs[:], in_=c.ps[3][:, 128:256], func=AF.Sqrt, bias=c.eps_t[:], scale=1.0 / 512),
                      reads=[c.ps_tr[3], c.eps_tr], writes=[yrs_tr])
                fw.op("dve", lambda e: e.reciprocal(out=yrs[:], in_=yrs[:]), reads=[yrs_tr], writes=[yrs_tr])
                for j in range(4):
                    fw.op("dve", lambda e, j=j, b=b: e.scalar_tensor_tensor(
                        out=ya[b][:, j, :], in0=yg[:, j, :], scalar=sng[:, g * 4 + j:g * 4 + j + 1], in1=yrs[:],
                        op0=ALU.mult, op1=ALU.mult), reads=[yg_tr, par_tr, yrs_tr], writes=[ya_tr[b]])
                fw.dma("sp", c.yaT[g * 512:(g + 1) * 512, csl].rearrange("(j p) t -> p j t", p=128), ya[b][:],
                       reads=[ya_tr[b]], writes=[c.yaT_tr])


def rope_tables(c):
    nc, fw, S = c.nc, c.fw, c.S
    c.rope_d = nc.dram_tensor("rope_d", [4, 128, S], F32).ap()
    c.rope_tr = TR("rope")
    TWO_PI = 2.0 * math.pi
    with ExitStack() as es:
        def sb(name, shape, dt):
            return es.enter_context(nc.sbuf_tensor(U(name), shape, dt))
        pos_i = sb("pos_i", [128, S], I32)
        pos_f = sb("pos_f", [128, S], F32)
        ang = sb("ang", [128, S], F32)
        kk = sb("kk", [128, S], F32)
        ki = sb("ki", [128, S], I32)
        res = sb("res", [128, S], F32)
        t_pos, t_ang, t_kk, t_ki, t_res = (TR() for _ in range(5))
        fw.dma("sp", pos_i[:], c.pos_in, writes=[t_pos])
        fw.op("dve", lambda e: e.tensor_copy(out=pos_f[:], in_=pos_i[:]), reads=[t_pos], writes=[t_pos])
        for ti, (icol, scol) in enumerate(((1024, 1025), (1026, 1027))):
            for which in (0, 1):
                shift = (math.pi / 2.0) if which == 0 else 0.0
                fw.op("dve", lambda e, icol=icol, shift=shift: e.tensor_scalar(
                    out=ang[:], in0=pos_f[:], scalar1=c.consts[:, icol:icol + 1], scalar2=shift, op0=ALU.mult, op1=ALU.add),
                    reads=[t_pos, c.const_tr], writes=[t_ang])
                fw.op("dve", lambda e: e.tensor_scalar(out=kk[:], in0=ang[:], scalar1=1.0 / TWO_PI, scalar2=None, op0=ALU.mult),
                      reads=[t_ang], writes=[t_kk])
                fw.op("dve", lambda e: e.tensor_copy(out=ki[:], in_=kk[:]), reads=[t_kk], writes=[t_ki])
                fw.op("dve", lambda e: e.tensor_copy(out=kk[:], in_=ki[:]), reads=[t_ki], writes=[t_kk])
                fw.op("dve", lambda e: e.scalar_tensor_tensor(out=ang[:], in0=kk[:], scalar=-TWO_PI, in1=ang[:], op0=ALU.mult, op1=ALU.add),
                      reads=[t_kk, t_ang], writes=[t_ang])
                fw.op("dve", lambda e: e.tensor_scalar(out=kk[:], in0=ang[:], scalar1=math.pi, scalar2=-TWO_PI, op0=ALU.is_gt, op1=ALU.mult),
                      reads=[t_ang], writes=[t_kk])
                fw.op("dve", lambda e: e.tensor_tensor(out=ang[:], in0=ang[:], in1=kk[:], op=ALU.add), reads=[t_ang, t_kk], writes=[t_ang])
                fw.op("dve", lambda e: e.tensor_scalar(out=kk[:], in0=ang[:], scalar1=-math.pi, scalar2=TWO_PI, op0=ALU.is_lt, op1=ALU.mult),
                      reads=[t_ang], writes=[t_kk])
                fw.op("dve", lambda e: e.tensor_tensor(out=ang[:], in0=ang[:], in1=kk[:], op=ALU.add), reads=[t_ang, t_kk], writes=[t_ang])
                fw.op("act", lambda e: e.activation(out=res[:], in_=ang[:], func=AF.Sin), reads=[t_ang], writes=[t_res])
                if which == 1:
                    fw.op("dve", lambda e, scol=scol: e.tensor_scalar(out=res[:], in0=res[:], scalar1=c.consts[:, scol:scol + 1], scalar2=None, op0=ALU.mult),
                          reads=[t_res, c.const_tr], writes=[t_res])
                fw.dma("sp", c.rope_d[ti * 2 + which], res[:], reads=[t_res], writes=[c.rope_tr])
    fw.barrier()


def dsa_stage(c, l):
    nc, fw, S, NT = c.nc, c.fw, c.S, c.NT
    NCK = S // 128
    TOPK = min(256, S // 4)
    JT = TOPK // 128
    scale = HEAD_DIM ** -0.5
    with ExitStack() as es:
        def sb(name, shape, dt):
            return es.enter_context(nc.sbuf_tensor(U(name), shape, dt))
        if getattr(c, "cut", 0) == 501:
            return
        kT = sb("kT", [128, 4, S], BF16)
        kT_tr = TR()
        v_sb = sb("v_sb", [128, NCK, 512], BF16)
        v_tr = TR()
        kiT = sb("kiT", [128, S], BF16)
        kiT_tr = TR()
        widx = sb("widx", [128, NCK, 16], F32)
        widx_tr = TR()
        ikn = sb("ikn", [128, 1], F32)
        ikn_tr = TR()
        fw.dma("sp", ikn[:], c.idx_k_norm[l], writes=[ikn_tr])
        with ExitStack() as es1:
            def sb1(name, shape, dt):
                return es1.enter_context(nc.sbuf_tensor(U(name), shape, dt))
            rope = sb1("rope", [128, 4, 512], F32)
            rope_tr = TR()
            wts = [sb1("dwt%d" % i, [128, NDC, 256], BF16) for i in range(2)]
            wts_tr = [TR(), TR()]
            wv = sb1("wv", [128, NDC, 512], BF16)
            wv_tr = TR()
            wwi = sb1("wwi", [128, NDC, 16], BF16)
            wwi_tr = TR()
            qb = [sb1("qb%d" % i, [128, 512], BF16) for i in range(2)]
            qb_tr = [TR(), TR()]
            t1 = [sb1("t1%d" % i, [128, 512], F32) for i in range(2)]
            t2 = [sb1("t2%d" % i, [128, 512], F32) for i in range(2)]
            t1_tr, t2_tr = [TR(), TR()], [TR(), TR()]
            qst = sb1("qst", [128, 24, 512], BF16)
            qst_tr = TR()
            kraw = sb1("kraw", [128, 512], F32)
            ksq = sb1("ksq", [128, 512], BF16)
            krs = sb1("krs", [128, 512], F32)
            kraw_tr, ksq_tr, krs_tr = TR(), TR(), TR()
            load_w(c, l, [(O_V, 512)], wv, wv_tr)
            load_w(c, l, [(O_WI, 16)], wwi, wwi_tr)
            for ck in range(NCK):
                pi = ck % 2
                for dc in range(NDC):
                    fw.op("pe", lambda e, dc=dc, ck=ck, pi=pi: e.matmul(
                        c.ps[pi][:], c.uT[:, dc, ck * 128:(ck + 1) * 128], wv[:, dc, :], start=(dc == 0), stop=(dc == NDC - 1)),
                        reads=[wv_tr, c.uT_tr[ck // 4]], writes=[c.ps_tr[pi]])
                fw.op("act", lambda e, ck=ck, pi=pi: e.copy(out=v_sb[:, ck, :], in_=c.ps[pi][:]), reads=[c.ps_tr[pi]], writes=[v_tr])
            for ck in range(NCK):
                for dc in range(NDC):
                    fw.op("pe", lambda e, dc=dc, ck=ck: e.matmul(
                        c.ps[2][:, ck * 16:(ck + 1) * 16], c.uT[:, dc, ck * 128:(ck + 1) * 128], wwi[:, dc, :], start=(dc == 0), stop=(dc == NDC - 1)),
                        reads=[wwi_tr, c.uT_tr[ck // 4]], writes=[c.ps_tr[2]])
            fw.op("act", lambda e: e.mul(out=widx[:], in_=c.ps[2][:, 0:NCK * 16].rearrange("p (k h) -> p k h", h=16), mul=(IDX_HEADS ** -0.5) * (IDX_DIM ** -0.5)),
                  reads=[c.ps_tr[2]], writes=[widx_tr])
            if getattr(c, "cut", 0) == 502:
                return
            chunks = [("q", hd, [(O_Q + hd * 128, 128)]) for hd in range(16)]
            chunks += [("k", g, [(O_K + g * 128, 128)]) for g in range(4)]
            chunks += [("qi", ci, [(O_QI + ci * 128, 128)]) for ci in range(8)]
            chunks += [("ki", 0, [(O_KI, 64), (O_KI, 64)])]
            wi = 0
            it = 0
            for tt in range(NT):
                tsl = slice(tt * 512, (tt + 1) * 512)
                for r in range(4):
                    fw.dma("sp", rope[:, r, :], c.rope_d[r, :, tsl], reads=[c.rope_tr], writes=[rope_tr])
                for ci in range(0, len(chunks), 2):
                    w = wi % 2
                    wi += 1
                    pair = chunks[ci:ci + 2]
                    segs = []
                    for (_, _, sg) in pair:
                        segs += sg
                    load_w(c, l, segs, wts[w], wts_tr[w])
                    for k, (kind, idx, _) in enumerate(pair):
                        b = it % 2
                        it += 1
                        pi = b
                        proj_fm(c, wts[w], wts_tr[w], k * 128, 128, tt, c.ps[pi][:], c.ps_tr[pi])
                        src_ap, src_tr = c.ps[pi][:], c.ps_tr[pi]
                        attn = kind in ("q", "k")
                        cosr, sinr = (0, 1) if attn else (2, 3)
                        P = c.PA_bf if attn else c.PI_bf
                        if kind == "ki":
                            fw.op("act", lambda e, pi=pi: e.copy(out=kraw[:], in_=c.ps[pi][:]), reads=[c.ps_tr[pi]], writes=[kraw_tr])
                            fw.op("act", lambda e: e.activation(out=ksq[:], in_=kraw[:], func=AF.Square), reads=[kraw_tr], writes=[ksq_tr])
                            fw.op("pe", lambda e: e.matmul(c.ps[3][:], c.ones_bf[0:64, :], ksq[0:64, :], start=True, stop=True),
                                  reads=[ksq_tr, c.const_tr], writes=[c.ps_tr[3]])
                            fw.op("act", lambda e: e.activation(out=krs[:], in_=c.ps[3][:], func=AF.Sqrt, bias=c.eps_t[:], scale=1.0 / IDX_DIM),
                                  reads=[c.ps_tr[3], c.eps_tr], writes=[krs_tr])
                            fw.op("dve", lambda e: e.reciprocal(out=krs[:], in_=krs[:]), reads=[krs_tr], writes=[krs_tr])
                            fw.op("dve", lambda e: e.scalar_tensor_tensor(out=kraw[:], in0=kraw[:], scalar=ikn[:, 0:1], in1=krs[:], op0=ALU.mult, op1=ALU.mult),
                                  reads=[kraw_tr, krs_tr, ikn_tr], writes=[kraw_tr])
                            src_ap, src_tr = kraw[:], kraw_tr
                        fw.op("act", lambda e, b=b, src_ap=src_ap: e.copy(out=qb[b][:], in_=src_ap), reads=[src_tr], writes=[qb_tr[b]])
                        pp = 4 + b
                        fw.op("pe", lambda e, b=b, pp=pp, P=P: e.matmul(c.ps[pp][:], P, qb[b][:], start=True, stop=True),
                              reads=[qb_tr[b], c.const_tr], writes=[c.ps_tr[pp]])
                        fw.op("dve", lambda e, b=b, src_ap=src_ap, cosr=cosr: e.tensor_tensor(out=t1[b][:], in0=src_ap, in1=rope[:, cosr, :], op=ALU.mult),
                              reads=[src_tr, rope_tr], writes=[t1_tr[b]])
                        fw.op("dve", lambda e, b=b, pp=pp, sinr=sinr: e.tensor_tensor(out=t2[b][:], in0=c.ps[pp][:], in1=rope[:, sinr, :], op=ALU.mult),
                              reads=[c.ps_tr[pp], rope_tr], writes=[t2_tr[b]])
                        if kind == "q":
                            dst, dtr = qst[:, idx, :], qst_tr
                        elif kind == "qi":
                            dst, dtr = qst[:, 16 + idx, :], qst_tr
                        elif kind == "k":
                            dst, dtr = kT[:, idx, tsl], kT_tr
                        else:
                            dst, dtr = kiT[:, tsl], kiT_tr
                        fw.op("pool", lambda e, b=b, dst=dst: e.tensor_tensor(out=dst, in0=t1[b][:], in1=t2[b][:], op=ALU.add),
                              reads=[t1_tr[b], t2_tr[b]], writes=[dtr])
                    if getattr(c, "cut", 0) == 503:
                        return
                    if getattr(c, "cut", 0) == 504 and ci == 26:
                        return
                fw.dma("sp", c.qT_d[:, tsl].rearrange("(h p) t -> p h t", p=128), qst[:, 0:16, :], reads=[qst_tr], writes=[c.qT_tr])
                fw.dma("sp", c.qiT_d[:, tsl].rearrange("(h p) t -> p h t", p=128), qst[:, 16:24, :], reads=[qst_tr], writes=[c.qT_tr])
        fw.barrier()
        if getattr(c, "cut", 0) == 51:
            return
        qblk = [sb("qblk%d" % i, [128, 16, 128], BF16) for i in range(2)]
        qiblk = [sb("qiblk%d" % i, [128, 8, 128], BF16) for i in range(2)]
        qblk_tr = [TR(), TR()]
        score = sb("score", [128, S], F32)
        score_tr = TR()
        junk = sb("junk", [128, S], BF16)
        junk_tr = TR()
        mask = sb("mask", [128, S], BF16)
        mask_tr = TR()
        maskT = sb("maskT", [128, NCK, 128], BF16)
        maskT_tr = TR()
        rl = [sb("rl%d" % i, [128, 512], F32) for i in range(2)]
        rl_tr = [TR(), TR()]
        pT = [sb("pT%d" % i, [128, 4, 128], BF16) for i in range(2)]
        pT_tr = [TR(), TR()]
        rinv = sb("rinv", [128, 512], F32)
        rinv_tr = TR()
        yb = [sb("yb%d" % i, [128, 16, 128], BF16) for i in range(2)]
        yb_tr = [TR(), TR()]
        hi = sb("hi", [128, 1], F32)
        lo = sb("lo", [128, 1], F32)
        mid = sb("mid", [128, 1], F32)
        cnt = sb("cnt", [128, 1], F32)
        selw = sb("selw", [128, 1], F32)
        Wt = sb("Wt", [128, NIT], F32)
        bis_tr = TR()
        lg_i = 0
        st_i = 0
        for j in range(NCK):
            b = j % 2
            L = 128 * (j + 1)
            jsl = slice(j * 128, (j + 1) * 128)
            fw.dma("sp", qblk[b][:], c.qT_d[:, jsl].rearrange("(h p) t -> p h t", p=128), reads=[c.qT_tr], writes=[qblk_tr[b]])
            fw.dma("sp", qiblk[b][:], c.qiT_d[:, jsl].rearrange("(h p) t -> p h t", p=128), reads=[c.qT_tr], writes=[qblk_tr[b]])
            if j < JT:
                if j > 0:
                    fw.op("pool", lambda e, j=j: e.memset(maskT[:, 0:j, :], 1.0), writes=[maskT_tr])
                fw.op("pool", lambda e, j=j: e.tensor_copy(out=maskT[:, j, :], in_=c.cbf[:, 640:768]), reads=[c.const_tr], writes=[maskT_tr])
            else:
                nkt = (L + 511) // 512
                for kt in range(nkt):
                    wk = min(512, L - kt * 512)
                    ksl = slice(kt * 512, kt * 512 + wk)
                    for h in range(IDX_HEADS):
                        pi = lg_i % 2
                        lg_i += 1
                        po = (h % 2) * 64
                        fw.op("pe", lambda e, b=b, h=h, po=po, pi=pi, ksl=ksl, wk=wk: e.matmul(
                            c.ps[pi][:, 0:wk], qiblk[b][po:po + 64, h // 2, :], kiT[po:po + 64, ksl], start=True, stop=True),
                            reads=[qblk_tr[b], kiT_tr], writes=[c.ps_tr[pi]])
                        fw.op("act", lambda e, pi=pi, wk=wk: e.activation(out=rl[pi][:, 0:wk], in_=c.ps[pi][:, 0:wk], func=AF.Relu),
                              reads=[c.ps_tr[pi]], writes=[rl_tr[pi]])
                        if h == 0:
                            fw.op("dve", lambda e, pi=pi, wk=wk, ksl=ksl, j=j: e.tensor_scalar(
                                out=score[:, ksl], in0=rl[pi][:, 0:wk], scalar1=widx[:, j, 0:1], scalar2=None, op0=ALU.mult),
                                reads=[rl_tr[pi], widx_tr], writes=[score_tr])
                        else:
                            fw.op("dve", lambda e, pi=pi, wk=wk, ksl=ksl, j=j, h=h: e.scalar_tensor_tensor(
                                out=score[:, ksl], in0=rl[pi][:, 0:wk], scalar=widx[:, j, h:h + 1], in1=score[:, ksl], op0=ALU.mult, op1=ALU.add),
                                reads=[rl_tr[pi], widx_tr, score_tr], writes=[score_tr])
                if getattr(c, "cut", 0) == 52:
                    return
                fw.op("dve", lambda e, jsl=jsl: e.tensor_tensor(out=score[:, jsl], in0=score[:, jsl], in1=c.consts[:, 512:640], op=ALU.add),
                      reads=[score_tr, c.const_tr], writes=[score_tr])
                fw.op("dve", lambda e, L=L: e.reduce_max(out=hi[:], in_=score[:, 0:L], axis=AX.X), reads=[score_tr], writes=[bis_tr])
                fw.op("dve", lambda e, j=j: e.tensor_reduce(out=lo[:], in_=score[:, 0:j * 128], axis=AX.X, op=ALU.min), reads=[score_tr], writes=[bis_tr])
                fw.op("dve", lambda e: e.tensor_tensor(out=mid[:], in0=hi[:], in1=lo[:], op=ALU.subtract), reads=[bis_tr], writes=[bis_tr])
                fw.op("dve", lambda e: e.tensor_scalar(out=Wt[:], in0=c.consts[:, 1028:1028 + NIT], scalar1=mid[:, 0:1], scalar2=None, op0=ALU.mult),
                      reads=[bis_tr, c.const_tr], writes=[bis_tr])
                for k in range(NIT):
                    fw.op("dve", lambda e, k=k: e.tensor_tensor(out=mid[:], in0=lo[:], in1=Wt[:, k:k + 1], op=ALU.add), reads=[bis_tr], writes=[bis_tr])
                    fw.op("dve", lambda e, L=L: e.tensor_scalar(out=junk[:, 0:L], in0=score[:, 0:L], scalar1=mid[:, 0:1], scalar2=None,
                                                                op0=ALU.is_ge, op1=ALU.add, accum_out=cnt[:]),
                          reads=[score_tr, bis_tr], writes=[junk_tr, bis_tr])
                    fw.op("dve", lambda e, k=k: e.tensor_scalar(out=selw[:], in0=cnt[:], scalar1=float(TOPK) - 0.5, scalar2=Wt[:, k:k + 1],
                                                                op0=ALU.is_ge, op1=ALU.mult), reads=[bis_tr], writes=[bis_tr])
                    fw.op("dve", lambda e: e.tensor_tensor(out=lo[:], in0=lo[:], in1=selw[:], op=ALU.add), reads=[bis_tr], writes=[bis_tr])
                fw.op("dve", lambda e, L=L: e.tensor_scalar(out=mask[:, 0:L], in0=score[:, 0:L], scalar1=lo[:, 0:1], scalar2=None, op0=ALU.is_ge),
                      reads=[score_tr, bis_tr], writes=[mask_tr])
                if getattr(c, "cut", 0) == 53:
                    return
                for c0 in range(0, j + 1, 8):
                    n = min(8, j + 1 - c0)
                    for cc in range(n):
                        fw.op("pe", lambda e, c0=c0, cc=cc: e.transpose(c.ps_bf[:, cc * 128:(cc + 1) * 128], mask[:, (c0 + cc) * 128:(c0 + cc + 1) * 128], c.ident_bf),
                              reads=[mask_tr, c.const_tr], writes=[c.ps_tr[7]])
                    fw.op("dve", lambda e, c0=c0, n=n: e.tensor_copy(out=maskT[:, c0:c0 + n, :], in_=c.ps_bf[:, 0:n * 128].rearrange("p (k t) -> p k t", t=128)),
                          reads=[c.ps_tr[7]], writes=[maskT_tr])
            if getattr(c, "cut", 0) == 54 and j == 1:
                return
            for g in range(4):
                for cc in range(j + 1):
                    si = 2 + st_i % 2
                    pb = st_i % 2
                    st_i += 1
                    csl = slice(cc * 128, (cc + 1) * 128)
                    fw.op("pe", lambda e, b=b, g=g, si=si, csl=csl: e.matmul(
                        c.ps[si][:], kT[:, g, csl], qblk[b][:, 4 * g:4 * g + 4, :].rearrange("p h t -> p (h t)"), start=True, stop=True),
                        reads=[kT_tr, qblk_tr[b]], writes=[c.ps_tr[si]])
                    fw.op("act", lambda e, si=si, pb=pb: e.activation(out=pT[pb][:], in_=c.ps[si][:].rearrange("p (h t) -> p h t", t=128), func=AF.Exp, scale=scale),
                          reads=[c.ps_tr[si]], writes=[pT_tr[pb]])
                    fw.op("dve", lambda e, pb=pb, cc=cc: e.tensor_tensor(
                        out=pT[pb][:], in0=pT[pb][:], in1=maskT[:, cc, :].unsqueeze(1).to_broadcast([128, 4, 128]), op=ALU.mult),
                        reads=[pT_tr[pb], maskT_tr], writes=[pT_tr[pb]])
                    fw.op("pe", lambda e, pb=pb, cc=cc, g=g, j=j: e.matmul(
                        c.ps[4][:], v_sb[:, cc, g * 128:(g + 1) * 128], pT[pb][:].rearrange("p h t -> p (h t)"), start=(cc == 0), stop=(cc == j)),
                        reads=[v_tr, pT_tr[pb]], writes=[c.ps_tr[4]])
                    fw.op("pe", lambda e, pb=pb, cc=cc, j=j: e.matmul(
                        c.ps[5][:], c.ones_bf, pT[pb][:].rearrange("p h t -> p (h t)"), start=(cc == 0), stop=(cc == j)),
                        reads=[c.const_tr, pT_tr[pb]], writes=[c.ps_tr[5]])
                fw.op("dve", lambda e: e.reciprocal(out=rinv[:], in_=c.ps[5][:]), reads=[c.ps_tr[5]], writes=[rinv_tr])
                fw.op("dve", lambda e, b=b, g=g: e.tensor_tensor(
                    out=yb[b][:, 4 * g:4 * g + 4, :], in0=c.ps[4][:].rearrange("p (h t) -> p h t", t=128), in1=rinv[:].rearrange("p (h t) -> p h t", t=128), op=ALU.mult),
                    reads=[c.ps_tr[4], rinv_tr], writes=[yb_tr[b]])
            fw.dma("sp", c.ybT[:, jsl].rearrange("(h p) t -> p h t", p=128), yb[b][:], reads=[yb_tr[b]], writes=[c.ybT_tr])


def merge_stage(c, l):
    nc, fw, S, NT = c.nc, c.fw, c.S, c.NT
    with ExitStack() as es:
        def sb(name, shape, dt):
            return es.enter_context(nc.sbuf_tensor(U(name), shape, dt))
        ya = sb("m_ya", [128, 32, 512], BF16)
        ybt = sb("m_yb", [128, 16, 512], BF16)
        ya_tr, yb_tr = TR(), TR()
        wpa = [sb("m_wpa%d" % i, [128, 32, 128], BF16) for i in range(2)]
        wpb = [sb("m_wpb%d" % i, [128, 16, 128], BF16) for i in range(2)]
        wg = [sb("m_wg%d" % i, [128, NDC, 256], BF16) for i in range(2)]
        wo = [sb("m_wo%d" % i, [128, NDC, 128], BF16) for i in range(2)]
        wpa_tr, wpb_tr, wg_tr, wo_tr = ([TR(), TR()] for _ in range(4))
        merged = sb("merged", [128, NDC, 512], BF16)
        merged_tr = [TR() for _ in range(NDC)]
        sga = sb("sga", [128, 512], F32)
        sgb = sb("sgb", [128, 512], F32)
        m1 = sb("m1", [128, 512], F32)
        m2 = sb("m2", [128, 512], F32)
        sga_tr, sgb_tr, m1_tr, m2_tr = TR(), TR(), TR(), TR()
        xo = sb("m_xo", [128, 2, 512], F32)
        xo_tr = [TR(), TR()]
        it = 0
        for tt in range(NT):
            tsl = slice(tt * 512, (tt + 1) * 512)
            fw.dma("sp", ya[:], c.yaT[:, tsl].rearrange("(k p) t -> p k t", p=128), reads=[c.yaT_tr], writes=[ya_tr])
            fw.dma("sp", ybt[:], c.ybT[:, tsl].rearrange("(k p) t -> p k t", p=128), reads=[c.ybT_tr], writes=[yb_tr])
            for dc in range(NDC):
                b = it % 2
                it += 1
                cols = slice(dc * 128, (dc + 1) * 128)
                fw.dma("sp", wpa[b][:], c.wpa_bf[l][:, cols].rearrange("(k p) f -> p k f", p=128), reads=[c.wtr[("wpa", l)]], writes=[wpa_tr[b]])
                fw.dma("sp", wpb[b][:], c.wpb_bf[l][:, cols].rearrange("(k p) f -> p k f", p=128), reads=[c.wtr[("wpb", l)]], writes=[wpb_tr[b]])
                load_w(c, l, [(O_GA + dc * 128, 128), (O_GB + dc * 128, 128)], wg[b], wg_tr[b])
                for k in range(32):
                    fw.op("pe", lambda e, k=k, b=b: e.matmul(c.ps[0][:], wpa[b][:, k, :], ya[:, k, :], start=(k == 0), stop=(k == 31)),
                          reads=[wpa_tr[b], ya_tr], writes=[c.ps_tr[0]])
                for k in range(16):
                    fw.op("pe", lambda e, k=k, b=b: e.matmul(c.ps[1][:], wpb[b][:, k, :], ybt[:, k, :], start=(k == 0), stop=(k == 15)),
                          reads=[wpb_tr[b], yb_tr], writes=[c.ps_tr[1]])
                proj_fm(c, wg[b], wg_tr[b], 0, 128, tt, c.ps[2][:], c.ps_tr[2])
                proj_fm(c, wg[b], wg_tr[b], 128, 128, tt, c.ps[3][:], c.ps_tr[3])
                fw.op("act", lambda e: e.activation(out=sga[:], in_=c.ps[2][:], func=AF.Sigmoid), reads=[c.ps_tr[2]], writes=[sga_tr])
                fw.op("act", lambda e: e.activation(out=sgb[:], in_=c.ps[3][:], func=AF.Sigmoid), reads=[c.ps_tr[3]], writes=[sgb_tr])
                fw.op("dve", lambda e: e.tensor_tensor(out=m1[:], in0=sga[:], in1=c.ps[0][:], op=ALU.mult), reads=[sga_tr, c.ps_tr[0]], writes=[m1_tr])
                fw.op("dve", lambda e: e.tensor_tensor(out=m2[:], in0=sgb[:], in1=c.ps[1][:], op=ALU.mult), reads=[sgb_tr, c.ps_tr[1]], writes=[m2_tr])
                fw.op("pool", lambda e, dc=dc: e.tensor_tensor(out=merged[:, dc, :], in0=m1[:], in1=m2[:], op=ALU.add), reads=[m1_tr, m2_tr], writes=[merged_tr[dc]])
            for do in range(NDC):
                b = do % 2
                po = 4 + do % 2
                cols = slice(do * 128, (do + 1) * 128)
                fw.dma("sp", wo[b][:], c.wout_bf[l][:, cols].rearrange("(k p) f -> p k f", p=128), reads=[c.wtr[("wout", l)]], writes=[wo_tr[b]])
                fw.dma("sp", xo[:, b, :], c.xT[do * 128:(do + 1) * 128, tsl], reads=[c.xT_tr[do][tt]], writes=[xo_tr[b]])
                for k in range(NDC):
                    fw.op("pe", lambda e, k=k, b=b, po=po: e.matmul(c.ps[po][:], wo[b][:, k, :], merged[:, k, :], start=(k == 0), stop=(k == NDC - 1)),
                          reads=[wo_tr[b], merged_tr[k]], writes=[c.ps_tr[po]])
                fw.op("dve", lambda e, b=b, po=po: e.tensor_tensor(out=xo[:, b, :], in0=xo[:, b, :], in1=c.ps[po][:], op=ALU.add),
                      reads=[xo_tr[b], c.ps_tr[po]], writes=[xo_tr[b]])
                fw.dma("sp", c.xT[do * 128:(do + 1) * 128, tsl], xo[:, b, :], reads=[xo_tr[b]], writes=[c.xT_tr[do][tt]])


_CACHE = {}


def kernel(**inputs):
    B, S = inputs["x"].shape[:2]
    NCORES = 8
    if "nc" not in _CACHE:
        _CACHE["nc"] = build_program(S=S, ncores=NCORES)
    nc, c = _CACHE["nc"]
    in_maps = [prep_inputs(inputs, core % B, S, core=core, ncores=NCORES) for core in range(NCORES)]
    res = run_bass_kernel_spmd(nc, in_maps, core_ids=list(range(NCORES)))
    out = np.stack([np.ascontiguousarray(res.results[b]["outT"].T) for b in range(B)], axis=0)
    return out.astype(np.float32)
```
